# Optimizing a Trainium2 kernel written in Bass

```python
import jax, jax.numpy as jnp
from jax import lax
import numpy as np

D_MODEL = 2048
BATCH = 4
SEQ = 2048
DEPTH = 2
DEC_BATCH = 128
DEC_SEQ = 8
PAST_LEN = 16384
PAGE_SIZE = 128

D_MIX = D_MODEL
N_GROUPS = 4
W_GROUP = D_MIX // N_GROUPS
A_HEADS = 4
A_HD = W_GROUP // A_HEADS
A_CHUNK = 128
R_HEADS = 4
R_HD = W_GROUP // R_HEADS
R_CHUNK = 128
ROPE_BASE = 10000.0
C_WIDTH = 3
K_HD = 64
K_HEADS = W_GROUP // K_HD
W_LORA = D_MODEL // 32
A_LORA = D_MODEL // 32
G_LORA = D_MODEL // 16
D_FF = 256 * ((8 * D_MODEL // 3 + 255) // 256)
EPS = 1e-6
GN_EPS = 64e-5
A_COLS = 2 * W_GROUP
R_COLS = 4 * W_GROUP
C_COLS = 3 * W_GROUP
K_COLS = 3 * W_GROUP + W_LORA + A_LORA + G_LORA
N_COLS = A_COLS + R_COLS + C_COLS + K_COLS

kernel_name = 'hybrid_parallel_groups_decode_step'

F32 = jnp.float32


def rmsnorm(x, g):
    xf = x.astype(F32)
    y = xf * lax.rsqrt(jnp.mean(xf * xf, axis=-1, keepdims=True) + EPS) * g.astype(F32)
    return y.astype(x.dtype)


def swiglu(x, w_gate, w_up, w_down):
    return (jax.nn.silu(x @ w_gate) * (x @ w_up)) @ w_down


def rope(x, pos):
    half = x.shape[-1] // 2
    inv = ROPE_BASE ** (-jnp.arange(half, dtype=F32) / half)
    ang = pos.astype(F32)[:, None] * inv[None, :]
    cos = jnp.cos(ang)[None, :, None, :]
    sin = jnp.sin(ang)[None, :, None, :]
    x1, x2 = x[..., :half], x[..., half:]
    return jnp.concatenate([x1 * cos - x2 * sin, x1 * sin + x2 * cos], axis=-1)


def chunk_spatial_gating(pa, w_s, b_s, ln_g, ln_b):
    bsz, L, _ = pa.shape
    u, v = jnp.split(jax.nn.gelu(pa.astype(F32)), 2, axis=-1)
    mu = jnp.mean(v, axis=-1, keepdims=True)
    var = jnp.mean(jnp.square(v - mu), axis=-1, keepdims=True)
    v = (v - mu) * lax.rsqrt(var + EPS) * ln_g.astype(F32) + ln_b.astype(F32)
    cl = min(A_CHUNK, L)
    n = L // cl
    wm = jnp.tril(w_s.astype(F32)[:, :cl, :cl])
    vh = v.reshape(bsz, n, cl, A_HEADS, A_HD)
    z = jnp.einsum('hts,bnshd->bnthd', wm, vh) + b_s.astype(F32)[:, :cl].T[None, None, :, :, None]
    out = u * z.reshape(bsz, L, W_GROUP)
    return out.astype(pa.dtype), v.astype(pa.dtype)


def retention(pr, pos, s0):
    q, k, v, g = jnp.split(pr.astype(F32), 4, axis=-1)
    bsz, L, _ = q.shape
    q = rope(q.reshape(bsz, L, R_HEADS, R_HD), pos)
    k = rope(k.reshape(bsz, L, R_HEADS, R_HD), pos) * (R_HD ** -0.5)
    v = v.reshape(bsz, L, R_HEADS, R_HD)
    log_gamma = jnp.log(1.0 - 2.0 ** (-5.0 - jnp.arange(R_HEADS, dtype=F32)))
    cl = min(R_CHUNK, L)
    n = L // cl
    qc = q.reshape(bsz, n, cl, R_HEADS, R_HD)
    kc = k.reshape(bsz, n, cl, R_HEADS, R_HD)
    vc = v.reshape(bsz, n, cl, R_HEADS, R_HD)
    idx = jnp.arange(cl, dtype=F32)
    diff = idx[:, None] - idx[None, :]
    dmat = jnp.where(diff >= 0, jnp.exp(jnp.maximum(diff, 0.0)[None] * log_gamma[:, None, None]), 0.0)
    scores = jnp.einsum('bnihd,bnjhd->bnhij', qc, kc) * dmat
    intra = jnp.einsum('bnhij,bnjhe->bnihe', scores, vc)
    kdec = jnp.exp((cl - 1.0 - idx)[:, None] * log_gamma[None, :])
    kv = jnp.einsum('bnjhd,bnjhe,jh->nbhde', kc, vc, kdec)
    chunk_decay = jnp.exp(cl * log_gamma)[None, :, None, None]

    def step(s, kv_n):
        return chunk_decay * s + kv_n, s

    s_last, s_prev = lax.scan(step, s0.astype(F32), kv)
    qdec = jnp.exp((idx + 1.0)[:, None] * log_gamma[None, :])
    cross = jnp.einsum('bnihd,nbhde,ih->bnihe', qc, s_prev, qdec)
    o = intra + cross
    o = o * lax.rsqrt(jnp.mean(o * o, axis=-1, keepdims=True) + EPS)
    o = o.reshape(bsz, L, W_GROUP) * jax.nn.silu(g)
    return o.astype(pr.dtype), s_last


def short_conv(pc, buf, conv_w):
    bg, cg, h = jnp.split(pc, 3, axis=-1)
    z = cg * h
    L = z.shape[1]
    zp = jnp.concatenate([buf.astype(z.dtype), z], axis=1)
    w = conv_w.astype(z.dtype)
    y = w[0] * zp[:, 0:L]
    for j in range(1, C_WIDTH):
        y = y + w[j] * zp[:, j:j + L]
    return bg * y, zp[:, -(C_WIDTH - 1):]


def rwkv7(pk, shift, s0, mu, w0, w2, a0, a2, g2, k_k, k_a, r_k, ln_w, ln_b):
    bsz, L, _ = pk.shape
    pkf = pk.astype(F32)
    prev = jnp.concatenate([shift[:, None].astype(F32), pkf[:, :-1]], axis=1)
    xs = pkf + (prev - pkf) * mu.astype(F32)
    W = W_GROUP
    r, k, v, wl, al, gl = jnp.split(xs, [W, 2 * W, 3 * W, 3 * W + W_LORA, 3 * W + W_LORA + A_LORA], axis=-1)
    wlog = -jax.nn.softplus(-(w0.astype(F32) + jnp.tanh(wl) @ w2.astype(F32))) - 0.5
    decay = jnp.exp(-jnp.exp(wlog))
    a = jax.nn.sigmoid(a0.astype(F32) + al @ a2.astype(F32))
    g = jax.nn.sigmoid(gl) @ g2.astype(F32)
    hs = (bsz, L, K_HEADS, K_HD)
    kk = (k * k_k.astype(F32)).reshape(hs)
    kk = kk / jnp.maximum(jnp.sqrt(jnp.sum(kk * kk, axis=-1, keepdims=True)), 1e-12)
    k = k * (1.0 + (a - 1.0) * k_a.astype(F32))
    r_h, k_h, v_h = r.reshape(hs), k.reshape(hs), v.reshape(hs)
    w_h, a_h = decay.reshape(hs), a.reshape(hs)

    def step(s, inp):
        r_t, k_t, v_t, w_t, kk_t, a_t = inp
        sk = jnp.einsum('bhvk,bhk->bhv', s, kk_t)
        s = s * w_t[:, :, None, :] - sk[..., None] * (kk_t * a_t)[:, :, None, :] + v_t[..., None] * k_t[:, :, None, :]
        y_t = jnp.einsum('bhvk,bhk->bhv', s, r_t)
        return s, y_t

    seq_in = (jnp.moveaxis(r_h, 1, 0), jnp.moveaxis(k_h, 1, 0), jnp.moveaxis(v_h, 1, 0),
              jnp.moveaxis(w_h, 1, 0), jnp.moveaxis(kk, 1, 0), jnp.moveaxis(a_h, 1, 0))
    s_last, y = lax.scan(step, s0.astype(F32), seq_in)
    y = jnp.moveaxis(y, 0, 1)
    mu_y = jnp.mean(y, axis=-1, keepdims=True)
    var_y = jnp.mean(jnp.square(y - mu_y), axis=-1, keepdims=True)
    y = ((y - mu_y) * lax.rsqrt(var_y + GN_EPS)).reshape(bsz, L, W_GROUP) * ln_w.astype(F32) + ln_b.astype(F32)
    bonus = jnp.sum(r_h * k_h * r_k.astype(F32), axis=-1, keepdims=True) * v_h
    y = (y + bonus.reshape(bsz, L, W_GROUP)) * g
    return y.astype(pk.dtype), pk[:, -1], s_last


def trunk_layer(x, pos, ret_s0, conv_buf, rw_shift, rw_s0, p):
    x = x + 0.5 * swiglu(rmsnorm(x, p['ffn1_norm']), p['ffn1_w_gate'], p['ffn1_w_up'], p['ffn1_w_down'])
    h = rmsnorm(x, p['mix_norm'])
    proj = h @ p['w_in']
    pa, pr, pc, pk = jnp.split(proj, [A_COLS, A_COLS + R_COLS, A_COLS + R_COLS + C_COLS], axis=-1)
    ya, v_rows = chunk_spatial_gating(pa, p['a_w_s'], p['a_b_s'], p['a_ln_g'], p['a_ln_b'])
    yb, ret_s = retention(pr, pos, ret_s0)
    yc, conv_new = short_conv(pc, conv_buf, p['c_conv_w'])
    yd, shift_new, rw_s = rwkv7(pk, rw_shift, rw_s0, p['k_mu'], p['k_w0'], p['k_w2'], p['k_a0'], p['k_a2'],
                                p['k_g2'], p['k_k_k'], p['k_k_a'], p['k_r_k'], p['k_ln_w'], p['k_ln_b'])
    mix = jnp.concatenate([ya.astype(x.dtype), yb.astype(x.dtype), yc.astype(x.dtype), yd.astype(x.dtype)], axis=-1)
    x = x + mix @ p['w_out']
    x = x + 0.5 * swiglu(rmsnorm(x, p['ffn2_norm']), p['ffn2_w_gate'], p['ffn2_w_up'], p['ffn2_w_down'])
    return x, ret_s, conv_new, shift_new, rw_s, v_rows


def setup_inputs(seed: int = 0) -> dict:
    key = jax.random.key(seed)
    ks = jax.random.split(key, 40)
    nrm = lambda i, shape, s: jax.random.normal(ks[i], shape, F32) * s
    return {
        'x_prompt': nrm(0, (BATCH, SEQ, D_MODEL), 1.0),
        'x_sample': nrm(1, (DEC_BATCH, DEC_SEQ, D_MODEL), 1.0),
        'state_ret': nrm(2, (DEPTH, DEC_BATCH, R_HEADS, R_HD, R_HD), 0.1),
        'state_conv': nrm(3, (DEPTH, DEC_BATCH, C_WIDTH - 1, W_GROUP), 1.0),
        'state_rwkv_shift': nrm(4, (DEPTH, DEC_BATCH, K_COLS), 1.0),
        'state_rwkv': nrm(5, (DEPTH, DEC_BATCH, K_HEADS, K_HD, K_HD), 0.1),
        'ffn1_norm': 1.0 + nrm(6, (DEPTH, D_MODEL), 0.01),
        'ffn1_w_gate': nrm(7, (DEPTH, D_MODEL, D_FF), D_MODEL ** -0.5),
        'ffn1_w_up': nrm(8, (DEPTH, D_MODEL, D_FF), D_MODEL ** -0.5),
        'ffn1_w_down': nrm(9, (DEPTH, D_FF, D_MODEL), D_FF ** -0.5),
        'mix_norm': 1.0 + nrm(10, (DEPTH, D_MODEL), 0.01),
        'w_in': nrm(11, (DEPTH, D_MODEL, N_COLS), D_MODEL ** -0.5),
        'w_out': nrm(12, (DEPTH, D_MIX, D_MODEL), D_MIX ** -0.5),
        'a_w_s': nrm(13, (DEPTH, A_HEADS, A_CHUNK, A_CHUNK), A_CHUNK ** -0.5),
        'a_b_s': 1.0 + nrm(14, (DEPTH, A_HEADS, A_CHUNK), 0.01),
        'a_ln_g': 1.0 + nrm(15, (DEPTH, W_GROUP), 0.01),
        'a_ln_b': nrm(16, (DEPTH, W_GROUP), 0.01),
        'c_conv_w': nrm(17, (DEPTH, C_WIDTH, W_GROUP), C_WIDTH ** -0.5),
        'k_mu': jax.random.uniform(ks[18], (DEPTH, K_COLS), F32),
        'k_w0': jax.random.uniform(ks[19], (DEPTH, W_GROUP), F32, -6.0, -1.0),
        'k_w2': nrm(20, (DEPTH, W_LORA, W_GROUP), 0.1),
        'k_a0': nrm(21, (DEPTH, W_GROUP), 0.1),
        'k_a2': nrm(22, (DEPTH, A_LORA, W_GROUP), 0.1),
        'k_g2': nrm(23, (DEPTH, G_LORA, W_GROUP), G_LORA ** -0.5),
        'k_k_k': 0.85 + nrm(24, (DEPTH, W_GROUP), 0.02),
        'k_k_a': 1.0 + nrm(25, (DEPTH, W_GROUP), 0.02),
        'k_r_k': nrm(26, (DEPTH, K_HEADS, K_HD), 0.1),
        'k_ln_w': 1.0 + nrm(27, (DEPTH, W_GROUP), 0.01),
        'k_ln_b': nrm(28, (DEPTH, W_GROUP), 0.01),
        'ffn2_norm': 1.0 + nrm(29, (DEPTH, D_MODEL), 0.01),
        'ffn2_w_gate': nrm(30, (DEPTH, D_MODEL, D_FF), D_MODEL ** -0.5),
        'ffn2_w_up': nrm(31, (DEPTH, D_MODEL, D_FF), D_MODEL ** -0.5),
        'ffn2_w_down': nrm(32, (DEPTH, D_FF, D_MODEL), D_FF ** -0.5),
        'final_norm': 1.0 + nrm(33, (D_MODEL,), 0.01),
    }


def reference(x_prompt, x_sample, state_ret, state_conv, state_rwkv_shift, state_rwkv,
              ffn1_norm, ffn1_w_gate, ffn1_w_up, ffn1_w_down, mix_norm, w_in, w_out,
              a_w_s, a_b_s, a_ln_g, a_ln_b, c_conv_w,
              k_mu, k_w0, k_w2, k_a0, k_a2, k_g2, k_k_k, k_k_a, k_r_k, k_ln_w, k_ln_b,
              ffn2_norm, ffn2_w_gate, ffn2_w_up, ffn2_w_down, final_norm):
    bp = x_prompt.shape[0]
    pos_p = jnp.arange(x_prompt.shape[1], dtype=F32)
    pos_s = PAST_LEN + jnp.arange(x_sample.shape[1], dtype=F32)
    zero_ret = jnp.zeros((bp, R_HEADS, R_HD, R_HD), F32)
    zero_conv = jnp.zeros((bp, C_WIDTH - 1, W_GROUP), x_prompt.dtype)
    zero_shift = jnp.zeros((bp, K_COLS), x_prompt.dtype)
    zero_rw = jnp.zeros((bp, K_HEADS, K_HD, K_HD), F32)

    xp, xs = x_prompt, x_sample
    ret_p, ret_s, conv_p, conv_s, shift_p, shift_s, rw_p, rw_s, v_s = [], [], [], [], [], [], [], [], []
    for l in range(DEPTH):
        p = dict(ffn1_norm=ffn1_norm[l], ffn1_w_gate=ffn1_w_gate[l], ffn1_w_up=ffn1_w_up[l],
                 ffn1_w_down=ffn1_w_down[l], mix_norm=mix_norm[l], w_in=w_in[l], w_out=w_out[l],
                 a_w_s=a_w_s[l], a_b_s=a_b_s[l], a_ln_g=a_ln_g[l], a_ln_b=a_ln_b[l], c_conv_w=c_conv_w[l],
                 k_mu=k_mu[l], k_w0=k_w0[l], k_w2=k_w2[l], k_a0=k_a0[l], k_a2=k_a2[l], k_g2=k_g2[l],
                 k_k_k=k_k_k[l], k_k_a=k_k_a[l], k_r_k=k_r_k[l], k_ln_w=k_ln_w[l], k_ln_b=k_ln_b[l],
                 ffn2_norm=ffn2_norm[l], ffn2_w_gate=ffn2_w_gate[l], ffn2_w_up=ffn2_w_up[l],
                 ffn2_w_down=ffn2_w_down[l])
        xp, rp, cp, sp, wp, _ = trunk_layer(xp, pos_p, zero_ret, zero_conv, zero_shift, zero_rw, p)
        xs, rs, cs, ss, ws, vs = trunk_layer(xs, pos_s, state_ret[l], state_conv[l], state_rwkv_shift[l],
                                             state_rwkv[l], p)
        ret_p.append(rp.astype(x_prompt.dtype)); ret_s.append(rs.astype(state_ret.dtype))
        conv_p.append(cp.astype(x_prompt.dtype)); conv_s.append(cs.astype(state_conv.dtype))
        shift_p.append(sp.astype(x_prompt.dtype)); shift_s.append(ss.astype(state_rwkv_shift.dtype))
        rw_p.append(wp.astype(x_prompt.dtype)); rw_s.append(ws.astype(state_rwkv.dtype))
        v_s.append(vs.astype(x_sample.dtype))

    y_prompt = rmsnorm(xp, final_norm)
    y_sample = rmsnorm(xs, final_norm)
    return (y_prompt, y_sample,
            jnp.stack(ret_p), jnp.stack(ret_s),
            jnp.stack(conv_p), jnp.stack(conv_s),
            jnp.stack(shift_p), jnp.stack(shift_s),
            jnp.stack(rw_p), jnp.stack(rw_s),
            jnp.stack(v_s))
```

```python
import numpy as np
import concourse.bass as bass
import concourse.mybir as mybir
from concourse.bass_utils import run_bass_kernel_spmd

F32 = mybir.dt.float32
BF16 = mybir.dt.bfloat16
AF = mybir.ActivationFunctionType
ALU = mybir.AluOpType
AX = mybir.AxisListType

EPOCH = 24000


class Trk:
    __slots__ = ("name", "w", "r", "dsem", "dcnt")

    def __init__(self, name):
        self.name = name
        self.w = []
        self.r = []
        self.dsem = None
        self.dcnt = 0


class V:
    __slots__ = ("ap", "trk")

    def __init__(self, ap, trk):
        self.ap = ap
        self.trk = trk

    def __getitem__(self, idx):
        return V(self.ap[idx], self.trk)

    def re(self, pat, **kw):
        return V(self.ap.rearrange(pat, **kw), self.trk)

    def bc(self, dt):
        return V(self.ap.bitcast(dt), self.trk)


class Eng:
    def __init__(self, K, name, h):
        self.K = K
        self.name = name
        self.h = h
        self.sems = []
        self.count = 0
        self.waited = {}
        self.inflight = []


class Kern:
    def __init__(self, nc, same_engine_sync=True):
        self.nc = nc
        self.PE = Eng(self, "pe", nc.tensor)
        self.ACT = Eng(self, "act", nc.scalar)
        self.DVE = Eng(self, "dve", nc.vector)
        self.POOL = Eng(self, "pool", nc.gpsimd)
        self.SP = Eng(self, "sp", nc.sync)
        self.engs = [self.PE, self.ACT, self.DVE, self.POOL, self.SP]
        self.same_engine_sync = same_engine_sync
        self.nsem = 0
        self.out_events = []
        self.ntile = 0
        self.ninstr = 0

    def new_sem(self, name):
        self.nsem += 1
        return self.nc.alloc_semaphore(f"{name}_{self.nsem}")

    def tile(self, name, shape, dt):
        t = self.nc.alloc_sbuf_tensor(name, list(shape), dt)
        return V(t[tuple(slice(None) for _ in shape)], Trk(name))

    def psum(self, name, shape, dt=F32):
        t = self.nc.alloc_psum_tensor(name, list(shape), dt)
        return V(t[tuple(slice(None) for _ in shape)], Trk(name))

    def _ev_sem_val(self, ev):
        if ev[0] == "E":
            eng, seq = ev[1], ev[2]
            ep = (seq - 1) // EPOCH
            return eng.sems[ep], seq - ep * EPOCH
        return ev[1], ev[2]

    def _wait(self, eng, ev):
        if ev[0] == "E" and ev[1] is eng and not self.same_engine_sync:
            return
        if ev[0] == "E" and ev[1] is eng and eng is self.PE:
            return
        if ev[0] == "E" and ev[1] is eng and eng in (self.SP,):
            pass
        sem, val = self._ev_sem_val(ev)
        key = id(sem)
        if eng.waited.get(key, 0) >= val:
            return
        eng.waited[key] = val
        eng.h.wait_ge(sem, val)

    def _pre(self, eng, reads, writes):
        for v in reads:
            for ev in v.trk.w:
                self._wait(eng, ev)
        for v in writes:
            for ev in v.trk.w:
                self._wait(eng, ev)
            for ev in v.trk.r:
                self._wait(eng, ev)

    def _post(self, ev, reads, writes):
        for v in reads:
            trk = v.trk
            trk.r = [e for e in trk.r if not (e[0] == ev[0] and e[1] is ev[1])] + [ev]
        for v in writes:
            trk = v.trk
            trk.w = [ev]
            trk.r = []

    def _next_event(self, eng, inc):
        seq = eng.count + 1
        ep = (seq - 1) // EPOCH
        while len(eng.sems) <= ep:
            eng.sems.append(self.new_sem(eng.name))
        if inc:
            eng.count = seq
        return ("E", eng, seq), eng.sems[ep]

    def op(self, eng, fn, reads, writes, inc=True):
        self._pre(eng, reads, writes)
        ev, sem = self._next_event(eng, inc)
        ins = fn()
        self.ninstr += 1
        if inc:
            ins.then_inc(sem, 1)
        self._post(ev, reads, writes)
        return ins

    MAX_INFLIGHT = {"sp": 6, "pool": 4}

    def _throttle(self, q):
        lim = self.MAX_INFLIGHT.get(q.name, 4)
        while len(q.inflight) >= lim:
            ev0 = q.inflight[0]
            same = [e for e in q.inflight if e[1] is ev0[1]]
            q.inflight = [e for e in q.inflight if e[1] is not ev0[1]]
            self._wait(q, max(same, key=lambda e: e[2]))

    def dma(self, q, out, in_, is_output=False, **kw):
        self._throttle(q)
        if isinstance(out, V):
            v = out
            self._pre(q, [], [v])
            trk = v.trk
            if trk.dsem is None:
                trk.dsem = self.new_sem("d")
            trk.dcnt += 16
            q.h.dma_start(out=v.ap, in_=in_, **kw).then_inc(trk.dsem, 16)
            ev = ("D", trk.dsem, trk.dcnt)
            q.inflight.append(ev)
            self._post(ev, [], [v])
        else:
            v = in_
            self._pre(q, [v], [])
            trk = v.trk
            if trk.dsem is None:
                trk.dsem = self.new_sem("d")
            trk.dcnt += 16
            q.h.dma_start(out=out, in_=v.ap, **kw).then_inc(trk.dsem, 16)
            ev = ("D", trk.dsem, trk.dcnt)
            q.inflight.append(ev)
            self._post(ev, [v], [])
            if is_output:
                self.out_events.append(ev)
        self.ninstr += 1

    def finish(self):
        best = {}
        for ev in self.out_events:
            k = id(ev[1])
            if k not in best or best[k][2] < ev[2]:
                best[k] = ev
        for ev in best.values():
            self._wait(self.SP, ev)

    def mm(self, out, lhsT, rhs, start, stop, inc=None, **kw):
        if inc is None:
            inc = stop
        return self.op(self.PE, lambda: self.nc.tensor.matmul(out.ap, lhsT.ap, rhs.ap, start=start, stop=stop, **kw),
                       [lhsT, rhs], [out], inc=inc)

    def transpose(self, out, in_, ident, inc=True):
        return self.op(self.PE, lambda: self.nc.tensor.transpose(out.ap, in_.ap, ident.ap), [in_, ident], [out], inc=inc)

    def act(self, out, in_, func, bias=None, scale=None, accum=None, eng=None):
        kw = {}
        reads = [in_]
        if bias is not None:
            if isinstance(bias, V):
                kw["bias"] = bias.ap
                reads.append(bias)
            else:
                kw["bias"] = bias
        if scale is not None:
            if isinstance(scale, V):
                kw["scale"] = scale.ap
                reads.append(scale)
            else:
                kw["scale"] = scale
        writes = [out]
        if accum is not None:
            kw["accum_out"] = accum.ap
            writes.append(accum)
        return self.op(self.ACT, lambda: self.nc.scalar.activation(out.ap, in_.ap, func, **kw), reads, writes)

    def _s(self, s, reads):
        if isinstance(s, V):
            reads.append(s)
            return s.ap
        return s

    def tt(self, out, a, b, op, eng=None):
        eng = eng or self.DVE
        return self.op(eng, lambda: eng.h.tensor_tensor(out.ap, a.ap, b.ap, op), [a, b], [out])

    def ts(self, out, a, s1, s2, op0, op1=None, eng=None, accum=None):
        eng = eng or self.DVE
        reads = [a]
        s1a = self._s(s1, reads)
        s2a = self._s(s2, reads) if s2 is not None else None
        kw = {}
        writes = [out]
        if op1 is not None:
            kw["op1"] = op1
        if accum is not None:
            kw["accum_out"] = accum.ap
            writes.append(accum)
        return self.op(eng, lambda: eng.h.tensor_scalar(out.ap, a.ap, s1a, s2a, op0, **kw), reads, writes)

    def stt(self, out, a, s, b, op0, op1, eng=None):
        eng = eng or self.DVE
        reads = [a, b]
        sa = self._s(s, reads)
        return self.op(eng, lambda: eng.h.scalar_tensor_tensor(out.ap, a.ap, sa, b.ap, op0, op1), reads, [out])

    def copy(self, out, in_, eng=None):
        eng = eng or self.DVE
        if eng is self.ACT:
            return self.op(eng, lambda: self.nc.scalar.copy(out.ap, in_.ap), [in_], [out])
        return self.op(eng, lambda: eng.h.tensor_copy(out.ap, in_.ap), [in_], [out])

    def memset(self, out, val, eng=None):
        eng = eng or self.DVE
        return self.op(eng, lambda: eng.h.memset(out.ap, val), [], [out])

    def recip(self, out, in_):
        return self.op(self.DVE, lambda: self.nc.vector.reciprocal(out.ap, in_.ap), [in_], [out])

D = 2048
DFF = 5632
NKC = 16
NFC = 44
NCOLS = 6400
DEPTH = 2
SEQ = 2048
NS = 16
LS = 8
EPS = 1e-6
SLOT = 8192
NSLOT = 4
FF_PARTS = [(0, 16), (16, 16), (32, 12)]

VEC_SPECS = [
    ("ffn1_norm", 16), ("mix_norm", 16), ("ffn2_norm", 16), ("a_ln_g", 4), ("a_ln_b", 4),
    ("c_conv_w", 12), ("k_mu", 14), ("k_w0", 4), ("k_a0", 4), ("k_k_k", 4), ("k_k_a", 4),
    ("k_r_k", 4), ("k_ln_w", 4), ("k_ln_b", 4),
]

WEIGHT_SHAPES = {
    "ffn1_w_gate": (DEPTH, D, DFF), "ffn1_w_up": (DEPTH, D, DFF), "ffn1_w_down": (DEPTH, DFF, D),
    "ffn2_w_gate": (DEPTH, D, DFF), "ffn2_w_up": (DEPTH, D, DFF), "ffn2_w_down": (DEPTH, DFF, D),
    "w_in": (DEPTH, D, NCOLS), "w_out": (DEPTH, D, D),
}


class Cfg:
    def __init__(self, **kw):
        self.n_prompt_tiles = 16
        self.tiles_per_block = 4
        self.do_sample = True
        self.layers = [0, 1]
        self.do_ffn1 = True
        self.do_mix = True
        self.do_ffn2 = True
        self.same_engine_sync = True
        self.groups = "ABCD"
        self.__dict__.update(kw)


class Builder:
    def __init__(self, cfg):
        self.cfg = cfg
        self.nc = bass.Bass("TRN2", target_bir_lowering=False)
        self.K = Kern(self.nc, same_engine_sync=cfg.same_engine_sync)
        self.dram = {}
        self.blocks = []
        t = 0
        while t < cfg.n_prompt_tiles:
            n = min(cfg.tiles_per_block, cfg.n_prompt_tiles - t)
            self.blocks.append(("P", list(range(t, t + n))))
            t += n
        if cfg.do_sample:
            self.blocks.append(("S", [0]))
        self.TMAX = 128 * max(len(b[1]) for b in self.blocks)

    def din(self, name, shape, dt=F32):
        self.dram[name] = self.nc.dram_tensor(name, list(shape), dt, kind="ExternalInput").ap()
        return self.dram[name]

    def dout(self, name, shape, dt=F32):
        self.dram[name] = self.nc.dram_tensor(name, list(shape), dt, kind="ExternalOutput").ap()
        return self.dram[name]

    def declare(self):
        cfg = self.cfg
        NP = cfg.n_prompt_tiles * 128
        self.din("xp", [NP, D])
        self.dout("yp", [NP, D])
        if cfg.do_sample:
            self.din("xs", [NS * LS, D])
            self.dout("ys", [NS * LS, D])
        for n, s in WEIGHT_SHAPES.items():
            self.din(n, s)
        for n, r in VEC_SPECS:
            self.din(n, [DEPTH, r * 128])
        self.din("final_norm", [D])
        self.din("c_ident", [128, 128])
        self.declare_mix()

    def declare_mix(self):
        pass

    def alloc(self):
        K = self.K
        T = self.TMAX
        self.x = K.tile("x", [128, NKC, T], F32)
        self.xn = K.tile("xn", [128, NKC, T], BF16)
        self.h = K.tile("h", [128, 16, T], BF16)
        self.mT = self.h
        self.slots = [K.tile(f"wslot{i}", [128, SLOT], BF16) for i in range(NSLOT)]
        self.ps = [K.psum(f"ps{i}", [128, 512], F32) for i in range(8)]
        self.psb = [V(self.ps[i].ap.bitcast(BF16)[:, 0:512], self.ps[i].trk) for i in (6, 7)]
        self.S8x = [K.tile(f"S8_{i}", [128, 2056], F32) for i in range(5)]
        self.S8 = [t[:, 0:2048] for t in self.S8x]
        self.stage = [self.S8[3], self.S8[4]]
        self.sq = [K.tile(f"sq{i}", [128, T], BF16) for i in range(2)]
        self.sg = [K.tile("sg0", [128, T], BF16)] * 2
        self.rt = K.tile("rt", [128, T], F32)
        self.rstd = K.tile("rstd", [128, T], F32)
        self.ident = K.tile("ident", [128, 128], F32)
        self.identb = K.tile("identb", [128, 128], BF16)
        self.ones = K.tile("ones", [128, 128], BF16)
        nv = sum(r for _, r in VEC_SPECS) * DEPTH + 16
        self.vecs = K.tile("vecs", [128, nv], F32)
        self.voff = {}
        self.alloc_mix()

    def alloc_mix(self):
        pass

    def load_consts(self):
        K = self.K
        K.dma(K.SP, self.ident, self.dram["c_ident"][:, :])
        K.copy(self.identb, self.ident)
        K.memset(self.ones, 1.0)
        rows = []
        for l in range(DEPTH):
            for n, r in VEC_SPECS:
                rows.append(((n, l), self.dram[n][l].rearrange("(r c) -> r c", c=128), r))
        rows.append((("final_norm", 0), self.dram["final_norm"].rearrange("(r c) -> r c", c=128), 16))
        off = 0
        i = 0
        g = 0
        while i < len(rows):
            st = self.stage[g % 2]
            n0 = 0
            grp = []
            while i < len(rows) and n0 + rows[i][2] <= 128:
                key, ap, r = rows[i]
                K.dma(K.SP, st[n0:n0 + r, 0:128], ap)
                grp.append((key, n0, r))
                n0 += r
                i += 1
            pt = self.ps[6 + g % 2]
            K.transpose(pt[:, 0:n0], st[0:n0, 0:128], self.ident[0:n0, 0:n0])
            K.copy(self.vecs[:, off:off + n0], pt[:, 0:n0])
            for key, o, r in grp:
                self.voff[key] = (off + o, r)
            off += n0
            g += 1
        self.load_consts_mix()

    def load_consts_mix(self):
        pass

    def vcol(self, name, l=0):
        o, r = self.voff[(name, l)]
        return self.vecs[:, o:o + r]

    def plan_weights(self):
        cfg = self.cfg
        q = []
        for b in self.blocks:
            for l in cfg.layers:
                if cfg.do_ffn1:
                    q += self.ffn_units(l, 1)
                if cfg.do_mix:
                    q += self.mix_units(l)
                if cfg.do_ffn2:
                    q += self.ffn_units(l, 2)
        self.wq = q
        self.wi = 0
        self.wissued = 0
        self.wdone = 0

    def ffn_units(self, l, which):
        q = []
        for part, (f0, nf) in enumerate(FF_PARTS):
            for u in range(f0 // 4, (f0 + nf) // 4):
                q.append(("g", l, which, u))
                q.append(("u", l, which, u))
            for oc4 in range(4):
                q.append(("d", l, which, part, oc4))
        return q

    def issue_w(self, idx):
        K = self.K
        desc = self.wq[idx]
        slot = self.slots[idx % NSLOT]
        kind = desc[0]
        if kind in ("g", "u"):
            _, l, which, u = desc
            w = self.dram[f"ffn{which}_w_{'gate' if kind == 'g' else 'up'}"]
            src = w[l, :, u * 512:(u + 1) * 512].rearrange("(k p) n -> p k n", p=128)
            dst = slot[:, 0:NKC * 512].re("p (k n) -> p k n", n=512)
        elif kind == "d":
            _, l, which, part, oc4 = desc
            f0, nf = FF_PARTS[part]
            w = self.dram[f"ffn{which}_w_down"]
            src = w[l, f0 * 128:(f0 + nf) * 128, oc4 * 512:(oc4 + 1) * 512].rearrange("(f p) n -> p f n", p=128)
            dst = slot[:, 0:nf * 512].re("p (f n) -> p f n", n=512)
        elif kind == "in":
            _, l, u = desc
            nco = min(512, NCOLS - u * 512)
            src = self.dram["w_in"][l, :, u * 512:u * 512 + nco].rearrange("(k p) n -> p k n", p=128)
            dst = slot[:, 0:NKC * nco].re("p (k n) -> p k n", n=nco)
        elif kind == "out":
            _, l, u = desc
            src = self.dram["w_out"][l, :, u * 512:(u + 1) * 512].rearrange("(k p) n -> p k n", p=128)
            dst = slot[:, 0:NKC * 512].re("p (k n) -> p k n", n=512)
        K.dma(K.POOL, dst, src)

    def pump(self):
        while self.wissued < len(self.wq) and self.wissued - NSLOT < self.wdone:
            self.issue_w(self.wissued)
            self.wissued += 1

    def get_w(self, desc):
        assert self.wq[self.wi] == desc, (self.wq[self.wi], desc)
        self.pump()
        assert self.wi < self.wissued, "weight unit not issuable (slot still held)"
        slot = self.slots[self.wi % NSLOT]
        self.wi += 1
        return slot

    def done_w(self, k=1):
        self.wdone += k
        self.pump()

    def rmsnorm(self, x, gcol, out, Tn):
        K = self.K
        pst = self.ps[6]
        for kc in range(NKC):
            sq = self.sq[kc % 2]
            K.act(sq[:, :Tn], x[:, kc, :Tn], AF.Square)
            K.mm(pst[:, :Tn], self.ones, sq[:, :Tn], start=(kc == 0), stop=(kc == NKC - 1), inc=True)
        K.act(self.rt[:, :Tn], pst[:, :Tn], AF.Sqrt, bias=self.epsc, scale=1.0 / D)
        K.recip(self.rstd[:, :Tn], self.rt[:, :Tn])
        for kc in range(NKC):
            K.stt(out[:, kc, :Tn], x[:, kc, :Tn], gcol[:, kc:kc + 1], self.rstd[:, :Tn], ALU.mult, ALU.mult)

    def ffn(self, l, which, Tn):
        K = self.K
        x, xn, h = self.x, self.xn, self.h
        self.rmsnorm(x, self.vcol(f"ffn{which}_norm", l), xn, Tn)
        ci = 0
        for part, (f0, nf) in enumerate(FF_PARTS):
            for u in range(f0 // 4, (f0 + nf) // 4):
                wg = self.get_w(("g", l, which, u))[:, 0:NKC * 512].re("p (k n) -> p k n", n=512)
                wu = self.get_w(("u", l, which, u))[:, 0:NKC * 512].re("p (k n) -> p k n", n=512)
                for fi in range(4):
                    f = 4 * u + fi
                    pg = self.ps[ci % 2]
                    pu = self.ps[2 + ci % 2]
                    sg = self.sg[ci % 2]
                    ci += 1
                    for kc in range(NKC):
                        K.mm(pg[:, :Tn], wg[:, kc, fi * 128:(fi + 1) * 128], xn[:, kc, :Tn], start=(kc == 0), stop=(kc == NKC - 1))
                    for kc in range(NKC):
                        K.mm(pu[:, :Tn], wu[:, kc, fi * 128:(fi + 1) * 128], xn[:, kc, :Tn], start=(kc == 0), stop=(kc == NKC - 1))
                    K.act(sg[:, :Tn], pg[:, :Tn], AF.Silu)
                    K.tt(h[:, f - f0, :Tn], sg[:, :Tn], pu[:, :Tn], ALU.mult)
                self.done_w(2)
            for oc4 in range(4):
                wd = self.get_w(("d", l, which, part, oc4))[:, 0:nf * 512].re("p (f n) -> p f n", n=512)
                for oci in range(4):
                    oc = 4 * oc4 + oci
                    pd = self.ps[4 + oc % 2]
                    for f in range(nf):
                        K.mm(pd[:, :Tn], wd[:, f, oci * 128:(oci + 1) * 128], h[:, f, :Tn], start=(f == 0), stop=(f == nf - 1))
                    K.stt(x[:, oc, :Tn], pd[:, :Tn], 0.5, x[:, oc, :Tn], ALU.mult, ALU.add)
                self.done_w(1)

    def load_x(self, blk):
        K = self.K
        kind, tiles = blk
        for ti, t in enumerate(tiles):
            st = self.stage[ti % 2]
            src = self.dram["xp"][t * 128:(t + 1) * 128, :] if kind == "P" else self.dram["xs"][:, :]
            K.dma(K.SP, st, src)
            for g in range(4):
                pt = self.ps[6 + g % 2]
                for j in range(4):
                    kc = 4 * g + j
                    K.transpose(pt[:, j * 128:(j + 1) * 128], st[:, kc * 128:(kc + 1) * 128], self.ident, inc=(j == 3))
                K.copy(self.x[:, 4 * g:4 * g + 4, ti * 128:(ti + 1) * 128], pt.re("p (j n) -> p j n", n=128), eng=(K.ACT if g % 2 else K.DVE))

    def store_y(self, blk, Tn):
        K = self.K
        kind, tiles = blk
        gcol = self.vcol("final_norm", 0)
        pst = self.ps[6]
        for kc in range(NKC):
            sq = self.sq[kc % 2]
            K.act(sq[:, :Tn], self.x[:, kc, :Tn], AF.Square)
            K.mm(pst[:, :Tn], self.ones, sq[:, :Tn], start=(kc == 0), stop=(kc == NKC - 1), inc=True)
        K.act(self.rt[:, :Tn], pst[:, :Tn], AF.Sqrt, bias=self.epsc, scale=1.0 / D)
        K.recip(self.rstd[:, :Tn], self.rt[:, :Tn])
        for ti, t in enumerate(tiles):
            yt = self.S8[ti % 2]
            st = self.stage[ti % 2]
            for g in range(4):
                pt = self.ps[6 + g % 2]
                for j in range(4):
                    kc = 4 * g + j
                    K.stt(yt[:, j * 128:(j + 1) * 128], self.x[:, kc, ti * 128:(ti + 1) * 128], gcol[:, kc:kc + 1],
                          self.rstd[:, ti * 128:(ti + 1) * 128], ALU.mult, ALU.mult)
                for j in range(4):
                    K.transpose(pt[:, j * 128:(j + 1) * 128], yt[:, j * 128:(j + 1) * 128], self.ident, inc=(j == 3))
                K.copy(st[:, g * 512:(g + 1) * 512], pt, eng=(K.ACT if g % 2 else K.DVE))
            dst = self.dram["yp"][t * 128:(t + 1) * 128, :] if kind == "P" else self.dram["ys"][:, :]
            K.dma(K.SP, dst, st, is_output=True)

    def mixer(self, blk, l, Tn):
        pass

    def build(self):
        cfg = self.cfg
        K = self.K
        self.declare()
        self.alloc()
        self.epsc = K.tile("epsc", [128, 1], F32)
        K.memset(self.epsc, EPS)
        self.load_consts()
        self.plan_weights()
        for blk in self.blocks:
            Tn = 128 * len(blk[1])
            self.load_x(blk)
            for l in cfg.layers:
                if cfg.do_ffn1:
                    self.ffn(l, 1, Tn)
                if cfg.do_mix:
                    self.mixer(blk, l, Tn)
                if cfg.do_ffn2:
                    self.ffn(l, 2, Tn)
            self.store_y(blk, Tn)
        K.finish()
        assert self.wi == len(self.wq)
        return self.nc


R_HEADS = 4
LG = [float(np.log(1.0 - 2.0 ** (-5.0 - h))) for h in range(R_HEADS)]
CONST_LAYOUT = [("DT_p", 512), ("DT_s", 512), ("QD_p", 512), ("QD_s", 512), ("KD_p", 4), ("KD_s", 4), ("SEQM", 16), ("TRIL", 128)]
RES_LAYOUT = [("DT", 512), ("QD", 512), ("KD", 4), ("SEQM", 16), ("TRIL", 128)]
PAST_LEN = 16384


def host_consts():
    out = {}
    j = np.arange(128)
    i = np.arange(128)
    lg = np.array(LG, dtype=np.float64)
    diff = (i[None, :] - j[:, None]).astype(np.float64)
    sc = 128.0 ** -0.5
    DTp = np.where(diff[:, None, :] >= 0, np.exp(np.maximum(diff, 0)[:, None, :] * lg[None, :, None]), 0.0) * sc
    same = (j[:, None] // 8) == (i[None, :] // 8)
    DTs = DTp * same[:, None, :]
    QDp = np.broadcast_to(np.exp((i[None, None, :] + 1.0) * lg[None, :, None]), (128, 4, 128))
    QDs = np.broadcast_to(np.exp(((i % 8)[None, None, :] + 1.0) * lg[None, :, None]), (128, 4, 128))
    KDp = np.exp((127.0 - j)[:, None] * lg[None, :]) * sc
    KDs = np.exp((7.0 - (j % 8))[:, None] * lg[None, :]) * sc
    SEQM = ((j[:, None] // 8) == np.arange(16)[None, :]).astype(np.float64)
    TRIL = (i[None, :] <= j[:, None]).astype(np.float64)
    parts = {"DT_p": DTp.reshape(128, 512), "DT_s": DTs.reshape(128, 512), "QD_p": QDp.reshape(128, 512),
             "QD_s": QDs.reshape(128, 512), "KD_p": KDp, "KD_s": KDs, "SEQM": SEQM, "TRIL": TRIL}
    out["c_consts"] = np.ascontiguousarray(np.concatenate([parts[n] for n, _ in CONST_LAYOUT], axis=1).astype(np.float32))
    half = 64
    inv = (np.float32(10000.0) ** (-(np.arange(half, dtype=np.float32) / np.float32(half)))).astype(np.float32)
    posp = np.arange(SEQ, dtype=np.float32)
    angp = posp[:, None] * inv[None, :]
    rp = np.concatenate([np.cos(angp), np.sin(angp)], axis=1).astype(np.float32)
    out["c_rope_p"] = np.ascontiguousarray(rp.reshape(16, 128, 128))
    poss = (np.float32(PAST_LEN) + (np.arange(128) % 8).astype(np.float32)).astype(np.float32)
    angs = poss[:, None] * inv[None, :]
    out["c_rope_s"] = np.ascontiguousarray(np.concatenate([np.cos(angs), np.sin(angs)], axis=1).astype(np.float32))
    out["c_ident"] = np.eye(128, dtype=np.float32)
    masks = []
    for blk in (64, 8):
        sameb = (j[:, None] // blk) == (i[None, :] // blk)
        masks += [((j[:, None] < i[None, :]) & sameb), ((j[:, None] <= i[None, :]) & sameb), ((j[:, None] > i[None, :]) & sameb)]
    for blk in (64, 8):
        masks.append(np.broadcast_to((i % blk != 0)[None, :], (128, 128)))
    masks.append((j[:, None] // 64) == (i[None, :] // 64))
    out["c_masks"] = np.ascontiguousarray(np.concatenate([mm_.astype(np.float32) for mm_ in masks], axis=1))
    mx = np.zeros((128, 272), np.float32)
    for hh in range(2):
        mx[:, hh * 128:(hh + 1) * 128] = ((i // 64) == hh)[None, :]
        mx[:, 256 + hh] = ((j // 64) == hh)
        mx[:, 258 + hh] = ((j // 64) == hh)
    out["c_maskx"] = mx
    return out


class Builder2(Builder):
    def declare_mix(self):
        cfg = self.cfg
        nconst = sum(n for _, n in CONST_LAYOUT)
        self.din("c_consts", [128, nconst])
        self.din("c_rope_p", [16, 128, 128])
        self.din("c_rope_s", [128, 128])
        self.din("a_w_s", [DEPTH, 4, 128, 128])
        self.din("a_b_s", [DEPTH, 4, 128])
        self.dout("ret_p", [DEPTH, 4, 128, 128])
        self.dout("conv_p", [DEPTH, 2, 512])
        if cfg.do_sample:
            self.din("state_ret", [DEPTH, NS, 4, 128, 128])
            self.din("state_conv", [DEPTH, NS, 2, 512])
            self.dout("ret_s", [DEPTH, NS, 4, 128, 128])
            self.dout("conv_s", [DEPTH, NS, 2, 512])
            self.dout("v_s", [DEPTH, NS * LS, 512])
        if "D" in cfg.groups:
            self.declare_D()

    def alloc_mix(self):
        K = self.K
        nconst = sum(n for _, n in CONST_LAYOUT)
        self.consts = K.tile("consts", [128, sum(n for _, n in RES_LAYOUT)], BF16)
        self.coff = {}
        self.doff = {}
        o = 0
        for n, w in RES_LAYOUT:
            self.coff[n] = (o, w)
            o += w
        o = 0
        for n, w in CONST_LAYOUT:
            self.doff[n] = (o, w)
            o += w
        self.const_kind = None
        self.wmT = K.tile("wmT", [128, 4, 128], BF16)
        self.arow = self.S8[2][:, 0:1536].re("p (r c) -> p r c", r=3)
        self.rope = self.rstd[:, 0:512].re("p (n c) -> p n c", n=4)
        self.Sret = K.tile("Sret", [128, DEPTH, 4, 128], F32)
        self.Sretb = K.tile("Sretb", [128, DEPTH, 4, 128], BF16)
        self.zc = K.tile("zc", [128, DEPTH, 4, 2], F32)
        self.st1 = K.tile("st1", [128, 8], F32)
        self.st2 = K.tile("st2", [128, 8], F32)
        self.gnepsc = K.tile("gnepsc", [128, 1], F32)
        self.dsm_out = self.rt[0:64, 0:512].re("p (c t) -> p c t", c=4)
        if "D" in self.cfg.groups:
            self.alloc_D()

    def cst(self, name):
        if name[-2:] in ("_p", "_s"):
            name = name[:-2]
        o, w = self.coff[name]
        return self.consts[:, o:o + w]

    def set_const_kind(self, kind):
        if self.const_kind == kind:
            return
        self.const_kind = kind
        sfx = "_p" if kind == "P" else "_s"
        for n in ("DT", "QD", "KD"):
            o, w = self.coff[n]
            do, _ = self.doff[n + sfx]
            self.K.dma(self.K.POOL, self.consts[:, o:o + w], self.dram["c_consts"][:, do:do + w])

    def load_consts_mix(self):
        K = self.K
        for n in ("SEQM", "TRIL"):
            o, w = self.coff[n]
            do, _ = self.doff[n]
            K.dma(K.POOL, self.consts[:, o:o + w], self.dram["c_consts"][:, do:do + w])
        K.memset(self.Sret, 0.0)
        K.memset(self.Sretb, 0.0)
        K.memset(self.zc, 0.0)
        K.memset(self.gnepsc, 64e-5)
        if "D" in self.cfg.groups:
            self.load_consts_D()

    def build_wmT(self, kind, l):
        K = self.K
        tril = self.cst("TRIL")
        for hd in range(4):
            raw = self.S8[3 + hd % 2][:, 0:128]
            msk = self.S8[3 + hd % 2][:, 128:256]
            if kind == "P":
                K.dma(K.SP, raw, self.dram["a_w_s"][l, hd, :, :])
            else:
                K.memset(raw, 0.0)
                for s in range(NS):
                    K.dma(K.SP, raw[8 * s:8 * s + 8, 8 * s:8 * s + 8], self.dram["a_w_s"][l, hd, 0:8, 0:8])
            K.tt(msk, raw, tril, ALU.mult)
            pt = self.ps[6 + hd % 2]
            K.transpose(pt[:, 0:128], msk, self.ident)
            K.copy(self.wmT[:, hd, :], pt[:, 0:128])

    def proj_fm(self, w, c, Tn, pt):
        K = self.K
        for kc in range(NKC):
            K.mm(pt[:, :Tn], w[:, kc, c * 128:(c + 1) * 128], self.xn[:, kc, :Tn], start=(kc == 0), stop=(kc == NKC - 1))

    def proj_tm(self, w, ti, ncols, pt):
        K = self.K
        for kc in range(NKC):
            K.mm(pt[:, :ncols], self.xn[:, kc, ti * 128:(ti + 1) * 128], w[:, kc, 0:ncols], start=(kc == 0), stop=(kc == NKC - 1))

    def wv(self, slot, ncols=512):
        return slot[:, 0:NKC * ncols].re("p (k n) -> p k n", n=ncols)

    def mix_A(self, blk, l, Tn):
        K = self.K
        kind, tiles = blk
        self.build_wmT(kind, l)
        K.dma(K.SP, self.arow[:, 0, :], self.dram["a_ln_g"][l:l + 1, :].broadcast_to([128, 512]))
        K.dma(K.SP, self.arow[:, 1, :], self.dram["a_ln_b"][l:l + 1, :].broadcast_to([128, 512]))
        if kind == "P":
            K.dma(K.SP, self.arow[:, 2, :], self.dram["a_b_s"][l:l + 1, :, :].rearrange("o h t -> o (h t)").broadcast_to([128, 512]))
        else:
            for hd in range(4):
                K.dma(K.SP, self.arow[:, 2, hd * 128:(hd + 1) * 128].re("p (s t) -> p s t", t=8),
                      self.dram["a_b_s"][l:l + 1, hd:hd + 1, 0:8].broadcast_to([128, NS, 8]))
        uT = self.S8[0].re("p (c t) -> p c t", c=4)
        w0 = self.wv(self.get_w(("in", l, 0)))
        for c in range(4):
            pt = self.ps[c % 2]
            self.proj_fm(w0, c, Tn, pt)
            K.act(uT[:, c, :Tn], pt[:, :Tn], AF.Gelu_apprx_tanh)
        self.done_w(1)
        w1 = self.wv(self.get_w(("in", l, 1)))
        sc = self.S8[1]
        for ti in range(len(tiles)):
            pt = self.ps[2 + ti % 2]
            self.proj_tm(w1, ti, 512, pt)
            vt = sc[:, 0:512]
            vh = sc[:, 512:1024]
            vsq = sc[:, 1024:1536]
            vb = sc[:, 1536:1792].bc(BF16)
            s1 = self.st1[:, 0:1]
            s2 = self.st1[:, 1:2]
            K.act(vt, pt, AF.Gelu_apprx_tanh, accum=s1)
            K.act(vsq, vt, AF.Square, accum=s2)
            mean = self.st2[:, 0:1]
            K.ts(mean, s1, 1.0 / 512, None, ALU.mult)
            msq = self.st2[:, 1:2]
            K.tt(msq, mean, mean, ALU.mult)
            var = self.st2[:, 2:3]
            K.stt(var, s2, 1.0 / 512, msq, ALU.mult, ALU.subtract)
            sd = self.st2[:, 3:4]
            K.act(sd, var, AF.Sqrt, bias=self.epsc, scale=1.0)
            rs = self.st2[:, 4:5]
            K.recip(rs, sd)
            K.ts(vh, vt, mean, rs, ALU.subtract, ALU.mult)
            K.tt(vh, vh, self.arow[:, 0, :], ALU.mult)
            K.tt(vh, vh, self.arow[:, 1, :], ALU.add)
            if kind == "S":
                K.dma(K.SP, self.dram["v_s"][l, :, :], vh, is_output=True)
            K.copy(vb, vh, eng=K.ACT)
            pz = self.ps[4 + ti % 2]
            for hd in range(4):
                K.mm(pz[:, hd * 128:(hd + 1) * 128], vb[:, hd * 128:(hd + 1) * 128], self.wmT[:, hd, :], start=True, stop=True, inc=(hd == 3))
            zt = sc[:, 0:512]
            K.tt(zt, pz, self.arow[:, 2, :], ALU.add)
            K.tt(self.mT[:, 0:4, ti * 128:(ti + 1) * 128], zt.re("p (h t) -> p h t", h=4), uT[:, 0:4, ti * 128:(ti + 1) * 128], ALU.mult)
        self.done_w(1)

    def mix_C(self, blk, l, Tn):
        K = self.K
        kind, tiles = blk
        cw = self.vcol("c_conv_w", l)
        cgT = self.S8[0].re("p (c t) -> p c t", c=4)
        zx = self.S8x[1]
        yT = self.S8[2].re("p (c t) -> p c t", c=4)
        w7 = self.wv(self.get_w(("in", l, 7)))
        for c in range(4):
            pt = self.ps[c % 2]
            self.proj_fm(w7, c, Tn, pt)
            K.copy(cgT[:, c, :Tn], pt[:, :Tn], eng=K.ACT)
        self.done_w(1)
        w8 = self.wv(self.get_w(("in", l, 8)))
        if kind == "P":
            zxv = zx[:, 0:4 * (Tn + 2)].re("p (c t) -> p c t", c=4)
            K.copy(zxv[:, :, 0:2], self.zc[:, l, :, :])
        else:
            zxv = zx[:, 0:4 * NS * 10].re("p (c s t) -> p c s t", c=4, s=NS)
            stg = self.S8[3]
            K.dma(K.SP, stg[0:2 * NS, 0:512], self.dram["state_conv"][l].rearrange("b j c -> (b j) c"))
            for c in range(4):
                pt = self.ps[6 + c % 2]
                K.transpose(pt[:, 0:2 * NS], stg[0:2 * NS, c * 128:(c + 1) * 128], self.ident[0:2 * NS, 0:2 * NS])
                K.copy(zxv[:, c, :, 0:2], pt[:, 0:2 * NS].re("p (s j) -> p s j", j=2))
        for c in range(4):
            pt = self.ps[c % 2]
            self.proj_fm(w8, c, Tn, pt)
            if kind == "P":
                K.tt(zxv[:, c, 2:2 + Tn], cgT[:, c, :Tn], pt[:, :Tn], ALU.mult)
            else:
                K.tt(zxv[:, c, :, 2:10], cgT[:, c, :Tn].re("p (s t) -> p s t", t=8), pt[:, :Tn].re("p (s t) -> p s t", t=8), ALU.mult)
        self.done_w(1)
        for c in range(4):
            if kind == "P":
                z0, z1, z2 = zxv[:, c, 0:Tn], zxv[:, c, 1:1 + Tn], zxv[:, c, 2:2 + Tn]
                yc = yT[:, c, :Tn]
            else:
                z0, z1, z2 = zxv[:, c, :, 0:8], zxv[:, c, :, 1:9], zxv[:, c, :, 2:10]
                yc = yT[:, c, :Tn].re("p (s t) -> p s t", t=8)
            K.ts(yc, z0, cw[:, c:c + 1], None, ALU.mult)
            K.stt(yc, z1, cw[:, 4 + c:5 + c], yc, ALU.mult, ALU.add)
            K.stt(yc, z2, cw[:, 8 + c:9 + c], yc, ALU.mult, ALU.add)
        if kind == "P":
            K.copy(self.zc[:, l, :, :], zxv[:, :, Tn:Tn + 2])
            if tiles[-1] == self.cfg.n_prompt_tiles - 1:
                self.out_rows_fm(self.zc[:, l, :, :], 2, self.dram["conv_p"][l], sample=False)
        else:
            self.out_rows_fm(zxv[:, :, :, 8:10], 2 * NS, self.dram["conv_s"][l].rearrange("b j c -> (b j) c"), sample=True)
        w6 = self.wv(self.get_w(("in", l, 6)))
        for c in range(4):
            pt = self.ps[c % 2]
            self.proj_fm(w6, c, Tn, pt)
            K.tt(self.mT[:, 8 + c, :Tn], yT[:, c, :Tn], pt[:, :Tn], ALU.mult)
        self.done_w(1)

    def out_rows_fm(self, src, nrows, dst, sample):
        K = self.K
        stg = self.S8[3]
        tmp = self.S8[4][:, 0:4 * nrows].re("p (c r) -> p c r", c=4)
        if sample:
            K.copy(tmp.re("p c (s j) -> p c s j", j=2), src)
        else:
            K.copy(tmp, src)
        for c in range(4):
            pt = self.ps[6 + c % 2]
            K.transpose(pt[0:nrows, 0:128], tmp[:, c, :], self.ident)
            K.copy(stg[0:nrows, c * 128:(c + 1) * 128], pt[0:nrows, 0:128])
        K.dma(K.SP, dst, stg[0:nrows, 0:512], is_output=True)


    def rope_tm(self, pt, out_bf, ropet, tmp):
        K = self.K
        p4 = pt.re("p (h two d) -> p h two d", two=2, d=64)
        o4 = out_bf.re("p (h two d) -> p h two d", two=2, d=64)
        cosb = V(ropet.ap[:, 0:64].unsqueeze(1).broadcast_to([128, 4, 64]), ropet.trk)
        sinb = V(ropet.ap[:, 64:128].unsqueeze(1).broadcast_to([128, 4, 64]), ropet.trk)
        t1 = tmp[:, 0:256].re("p (h d) -> p h d", d=64)
        t2 = tmp[:, 256:512].re("p (h d) -> p h d", d=64)
        x1 = p4[:, :, 0, :]
        x2 = p4[:, :, 1, :]
        K.tt(t1, x1, cosb, ALU.mult)
        K.tt(t2, x2, sinb, ALU.mult)
        K.tt(o4[:, :, 0, :], t1, t2, ALU.subtract)
        K.tt(t1, x1, sinb, ALU.mult)
        K.tt(t2, x2, cosb, ALU.mult)
        K.tt(o4[:, :, 1, :], t1, t2, ALU.add)

    def mix_B(self, blk, l, Tn):
        K = self.K
        kind, tiles = blk
        nt = len(tiles)
        sfx = "_p" if kind == "P" else "_s"
        self.set_const_kind(kind)
        s0b = self.S8[0].bc(BF16)
        qT = s0b[:, 0:2048].re("p (h t) -> p h t", h=4)
        kT = s0b[:, 2048:4096].re("p (h t) -> p h t", h=4)
        s1b = self.S8[1].bc(BF16)
        kdk = s1b[:, 0:2048].re("p (n c) -> p n c", c=512)
        vtk = s1b[:, 2048:4096].re("p (n c) -> p n c", c=512)
        oT = self.S8[2].re("p (h t) -> p h t", h=4)
        tmpf = self.S8[3]
        tmpb = self.S8[4].bc(BF16)
        if kind == "P":
            for ti, t in enumerate(tiles):
                K.dma(K.SP, self.rope[:, ti, :], self.dram["c_rope_p"][t, :, :])
        else:
            K.dma(K.SP, self.rope[:, 0, :], self.dram["c_rope_s"][:, :])
        kd = self.cst("KD" + sfx)
        w2 = self.wv(self.get_w(("in", l, 2)))
        for ti in range(nt):
            pt = self.ps[ti % 2]
            self.proj_tm(w2, ti, 512, pt)
            qb = tmpb[:, (ti % 2) * 512:(ti % 2 + 1) * 512]
            self.rope_tm(pt, qb, self.rope[:, ti, :], tmpf[:, (ti % 2) * 512:(ti % 2 + 1) * 512])
            ptb = self.psb[ti % 2]
            for hd in range(4):
                K.transpose(ptb[:, hd * 128:(hd + 1) * 128], qb[:, hd * 128:(hd + 1) * 128], self.identb, inc=(hd == 3))
            K.copy(qT[:, :, ti * 128:(ti + 1) * 128], ptb.re("p (h t) -> p h t", h=4), eng=K.ACT)
        self.done_w(1)
        w3 = self.wv(self.get_w(("in", l, 3)))
        for ti in range(nt):
            pt = self.ps[ti % 2]
            self.proj_tm(w3, ti, 512, pt)
            kb = tmpb[:, (ti % 2) * 512:(ti % 2 + 1) * 512]
            self.rope_tm(pt, kb, self.rope[:, ti, :], tmpf[:, (ti % 2) * 512:(ti % 2 + 1) * 512])
            ptb = self.psb[ti % 2]
            for hd in range(4):
                K.transpose(ptb[:, hd * 128:(hd + 1) * 128], kb[:, hd * 128:(hd + 1) * 128], self.identb, inc=(hd == 3))
            K.copy(kT[:, :, ti * 128:(ti + 1) * 128], ptb.re("p (h t) -> p h t", h=4), eng=K.ACT)
            kdb = V(kd.ap.unsqueeze(2).broadcast_to([128, 4, 128]), kd.trk)
            K.tt(kdk[:, ti, :].re("p (h d) -> p h d", h=4), kb.re("p (h d) -> p h d", h=4), kdb, ALU.mult)
        self.done_w(1)
        w4 = self.wv(self.get_w(("in", l, 4)))
        for ti in range(nt):
            pt = self.ps[ti % 2]
            self.proj_tm(w4, ti, 512, pt)
            K.copy(vtk[:, ti, :], pt, eng=K.ACT)
        self.done_w(1)
        DT = self.cst("DT" + sfx)
        QD = self.cst("QD" + sfx).re("p (h t) -> p h t", h=4)
        for ti in range(nt):
            cols = slice(ti * 128, (ti + 1) * 128)
            pS = self.ps[2]
            for hd in range(4):
                K.mm(pS[:, hd * 128:(hd + 1) * 128], kT[:, hd, cols], qT[:, hd, cols], start=True, stop=True, inc=(hd == 3))
            SDb = tmpb[:, 0:512]
            K.tt(SDb, pS, DT, ALU.mult)
            qd = tmpb[:, 512:1024].re("p (h t) -> p h t", h=4)
            K.tt(qd, qT[:, :, cols], QD, ALU.mult)
            pO = self.ps[3]
            if kind == "P":
                for hd in range(4):
                    K.mm(pO[:, hd * 128:(hd + 1) * 128], vtk[:, ti, hd * 128:(hd + 1) * 128], SDb[:, hd * 128:(hd + 1) * 128], start=True, stop=False, inc=False)
                    K.mm(pO[:, hd * 128:(hd + 1) * 128], self.Sretb[:, l, hd, :], qd[:, hd, :], start=False, stop=True, inc=(hd == 3))
                K.copy(oT[:, :, cols], pO.re("p (h t) -> p h t", h=4), eng=K.ACT)
                pU = self.ps[4]
                for hd in range(4):
                    K.mm(pU[:, hd * 128:(hd + 1) * 128], kdk[:, ti, hd * 128:(hd + 1) * 128], vtk[:, ti, hd * 128:(hd + 1) * 128], start=True, stop=True, inc=(hd == 3))
                for hd in range(4):
                    K.stt(self.Sret[:, l, hd, :], self.Sret[:, l, hd, :], float(np.exp(128.0 * LG[hd])), pU[:, hd * 128:(hd + 1) * 128], ALU.mult, ALU.add)
                K.copy(self.Sretb[:, l, :, :], self.Sret[:, l, :, :], eng=K.ACT)
                if tiles[ti] == self.cfg.n_prompt_tiles - 1:
                    K.dma(K.SP, self.dram["ret_p"][l].rearrange("h d e -> d h e"), self.Sret[:, l, :, :], is_output=True)
            else:
                for hd in range(4):
                    K.mm(pO[:, hd * 128:(hd + 1) * 128], vtk[:, ti, hd * 128:(hd + 1) * 128], SDb[:, hd * 128:(hd + 1) * 128], start=True, stop=True, inc=(hd == 3))
                K.copy(oT[:, :, cols], pO.re("p (h t) -> p h t", h=4), eng=K.ACT)
                seqm = self.cst("SEQM")
                for g in range(4):
                    S0 = self.S8[3].re("p (s h e) -> p s h e", s=4, h=4)
                    K.dma(K.SP, S0, self.dram["state_ret"][l, 4 * g:4 * g + 4].rearrange("s h d e -> d s h e"))
                    S0b = tmpb[:, 1024:3072].re("p (s h e) -> p s h e", s=4, h=4)
                    K.copy(S0b, S0, eng=K.ACT)
                    Vx = s0b_free = self.S8[4].bc(BF16)[:, 0:2048].re("p (s c) -> p s c", s=4) if False else None
                    pC = self.ps[5]
                    for hd in range(4):
                        for sl in range(4):
                            s = 4 * g + sl
                            K.mm(pC[:, hd * 128 + 8 * s: hd * 128 + 8 * s + 8], S0b[:, sl, hd, :], qd[:, hd, 8 * s:8 * s + 8], start=True, stop=True,
                                 inc=(hd == 3 and sl == 3))
                    Vx = tmpb[:, 3072:4096].re("p (s c) -> p s c", s=4) if False else None
                    for hd in range(4):
                        Vxh = tmpb[:, 3072:3584].re("p (s e) -> p s e", s=4)
                        vin = V(vtk.ap[:, ti, hd * 128:(hd + 1) * 128].unsqueeze(1).broadcast_to([128, 4, 128]), vtk.trk)
                        smb = V(seqm.ap[:, 4 * g:4 * g + 4].unsqueeze(2).broadcast_to([128, 4, 128]), seqm.trk)
                        K.tt(Vxh, vin, smb, ALU.mult)
                        pU = self.ps[4]
                        K.mm(pU, kdk[:, ti, hd * 128:(hd + 1) * 128], Vxh.re("p s e -> p (s e)"), start=True, stop=True)
                        K.stt(S0[:, :, hd, :], S0[:, :, hd, :], float(np.exp(8.0 * LG[hd])), pU.re("p (s e) -> p s e", s=4), ALU.mult, ALU.add)
                    K.dma(K.SP, self.dram["ret_s"][l, 4 * g:4 * g + 4].rearrange("s h d e -> d s h e"), S0, is_output=True)
                    c0 = 32 * g
                    K.tt(oT[:, :, c0:c0 + 32], oT[:, :, c0:c0 + 32], pC.re("p (h t) -> p h t", h=4)[:, :, c0:c0 + 32], ALU.add)
        w5 = self.wv(self.get_w(("in", l, 5)))
        for hd in range(4):
            sq = self.sq[hd % 2]
            K.act(sq[:, :Tn], oT[:, hd, :Tn], AF.Square)
            pst = self.ps[6 + hd % 2]
            K.mm(pst[:, :Tn], self.ones, sq[:, :Tn], start=True, stop=True)
            K.act(self.rt[:, :Tn], pst[:, :Tn], AF.Sqrt, bias=self.epsc, scale=1.0 / 128)
            K.recip(self.rstd[:, :Tn], self.rt[:, :Tn])
            pg = self.ps[hd % 2]
            self.proj_fm(w5, hd, Tn, pg)
            sg = self.sg[hd % 2]
            K.act(sg[:, :Tn], pg[:, :Tn], AF.Silu)
            K.tt(sg[:, :Tn], sg[:, :Tn], self.rstd[:, :Tn], ALU.mult)
            K.tt(self.mT[:, 4 + hd, :Tn], oT[:, hd, :Tn], sg[:, :Tn], ALU.mult)
        self.done_w(1)

    def wout(self, l, Tn):
        K = self.K
        for u in range(4):
            w = self.wv(self.get_w(("out", l, u)))
            for ci in range(4):
                oc = 4 * u + ci
                pd = self.ps[4 + oc % 2]
                for kc in range(NKC):
                    K.mm(pd[:, :Tn], w[:, kc, ci * 128:(ci + 1) * 128], self.mT[:, kc, :Tn], start=(kc == 0), stop=(kc == NKC - 1))
                K.stt(self.x[:, oc, :Tn], pd[:, :Tn], 1.0, self.x[:, oc, :Tn], ALU.mult, ALU.add)
            self.done_w(1)

    def mixer(self, blk, l, Tn):
        K = self.K
        cfg = self.cfg
        self.rmsnorm(self.x, self.vcol("mix_norm", l), self.xn, Tn)
        groups = cfg.groups
        if "A" in groups:
            self.mix_A(blk, l, Tn)
        else:
            K.memset(self.mT[:, 0:4, :Tn], 0.0)
        if "C" in groups:
            self.mix_C(blk, l, Tn)
        else:
            K.memset(self.mT[:, 8:12, :Tn], 0.0)
        if "B" in groups:
            self.mix_B(blk, l, Tn)
        else:
            K.memset(self.mT[:, 4:8, :Tn], 0.0)
        if "D" in groups:
            self.mix_D(blk, l, Tn)
        else:
            K.memset(self.mT[:, 12:16, :Tn], 0.0)
        self.wout(l, Tn)

    def mix_units(self, l):
        g = self.cfg.groups
        q = []
        if "A" in g:
            q += [("in", l, 0), ("in", l, 1)]
        if "C" in g:
            q += [("in", l, 7), ("in", l, 8), ("in", l, 6)]
        if "B" in g:
            q += [("in", l, 2), ("in", l, 3), ("in", l, 4), ("in", l, 5)]
        if "D" in g:
            q += [("in", l, u) for u in (9, 10, 11, 12)]
        q += [("out", l, u) for u in range(4)]
        return q


    def declare_D(self):
        cfg = self.cfg
        self.din("c_masks", [128, 9 * 128])
        self.din("c_maskx", [128, 272])
        self.din("k_w2", [DEPTH, 64, 512])
        self.din("k_a2", [DEPTH, 64, 512])
        self.din("k_g2", [DEPTH, 128, 512])
        self.dout("shift_p", [DEPTH, 1792])
        self.dout("rwkv_p", [DEPTH, 8, 64, 64])
        if cfg.do_sample:
            self.din("state_rwkv_shift", [DEPTH, NS, 1792])
            self.din("state_rwkv", [DEPTH, NS, 8, 64, 64])
            self.dout("shift_s", [DEPTH, NS, 1792])
            self.dout("rwkv_s", [DEPTH, NS, 8, 64, 64])

    def alloc_D(self):
        K = self.K
        self.dmask = K.tile("dmask", [128, 9, 128], BF16)
        self.LWA = K.tile("LWA", [128, DEPTH, 512], BF16)
        self.LG2 = K.tile("LG2", [128, DEPTH, 512], BF16)
        self.TD = K.tile("TD", [128, DEPTH, 4, 128], F32)
        self.TDb = K.tile("TDb", [128, DEPTH, 4, 128], BF16)
        self.shc = K.tile("shc", [128, DEPTH, 14], F32)
        names = ["X0", "XT0", "X1", "XT1", "PM", "A2T", "A3T", "nA4T", "KX", "RX", "BX", "VX", "UX"]
        self.DM = {n: K.tile("dm_" + n, [128, 2, 128], BF16) for n in names}
        self.DM["VS"] = K.tile("dm_VS", [128, 128], BF16)
        self.dmx = K.tile("dmx", [128, 272], BF16)
        self.dmx32 = K.tile("dmx32", [128, 2], F32)
        self.Wb = K.tile("Wb", [128, 128], BF16)
        self.Ub = K.tile("Ub", [128, 128], BF16)
        self.lt = K.tile("lt", [128, 3, 128], BF16)

    def load_consts_D(self):
        K = self.K
        K.dma(K.POOL, self.dmask, self.dram["c_masks"].rearrange("p (m t) -> p m t", m=9))
        for l in range(DEPTH):
            K.dma(K.POOL, self.LWA[0:64, l, :], self.dram["k_w2"][l])
            K.dma(K.POOL, self.LWA[64:128, l, :], self.dram["k_a2"][l])
            K.dma(K.POOL, self.LG2[:, l, :], self.dram["k_g2"][l])
        K.dma(K.POOL, self.dmx, self.dram["c_maskx"][:, :])
        K.dma(K.SP, self.dmx32, self.dram["c_maskx"][:, 258:260])
        K.memset(self.Wb, 0.0)
        K.memset(self.lt, 0.0)
        K.memset(self.Ub, 0.0)
        K.memset(self.TD, 0.0)
        K.memset(self.TDb, 0.0)
        K.memset(self.shc, 0.0)

    def bcast_mid(self, v, n):
        return V(v.ap.unsqueeze(1).broadcast_to([128, n, v.ap.shape[-1]]), v.trk)

    def mix_D(self, blk, l, Tn):
        K = self.K
        kind, tiles = blk
        nt = len(tiles)
        S = (kind == "S")
        mo = 3 if S else 0
        MU, MUI, ML = self.dmask[:, mo + 0, :], self.dmask[:, mo + 1, :], self.dmask[:, mo + 2, :]
        RESET = self.dmask[:, 7 if S else 6, :]
        BONES = self.dmask[:, 8, :]
        XA = self.S8[0].bc(BF16).re("p (c t) -> p c t", c=8)
        XB = self.S8[1].bc(BF16).re("p (c t) -> p c t", c=8)
        mu = self.vcol("k_mu", l)

        def xs_dst(c):
            if c < 8:
                return XA[:, c, :Tn]
            if c < 12:
                return XB[:, c - 8, :Tn]
            return XB[:, 4 + c - 12, :Tn]

        if S:
            shs = self.S8[4][:, 0:14 * NS].re("p (c b) -> p c b", c=14)
            sho = self.S8[4][:, 256:256 + 14 * NS].re("p (c b) -> p c b", c=14)
            stg = self.S8[3]
            K.dma(K.SP, stg[0:NS, 0:1792], self.dram["state_rwkv_shift"][l])
            for c in range(14):
                pt = self.ps[6 + c % 2]
                K.transpose(pt[:, 0:128], stg[:, c * 128:(c + 1) * 128], self.ident)
                K.copy(shs[:, c, :], pt[:, 0:NS])
        c = 0
        for u in (9, 10, 11, 12):
            ncol = 512 if u < 12 else 256
            w = self.wv(self.get_w(("in", l, u)), ncol)
            for ci in range(ncol // 128):
                pt = self.ps[c % 2]
                self.proj_fm(w, ci, Tn, pt)
                pkx = self.S8x[2 + c % 2]
                dd = self.S8[4][:, 512 + (c % 2) * 512: 1024 + (c % 2) * 512] if S else self.S8[4][:, (c % 2) * 1024:(c % 2) * 1024 + Tn]
                if not S:
                    K.copy(pkx[:, 1:1 + Tn], pt[:, :Tn], eng=K.ACT)
                    K.copy(pkx[:, 0:1], self.shc[:, l, c:c + 1])
                    K.tt(dd, pkx[:, 0:Tn], pkx[:, 1:1 + Tn], ALU.subtract)
                    K.stt(xs_dst(c), dd, mu[:, c:c + 1], pkx[:, 1:1 + Tn], ALU.mult, ALU.add)
                    K.copy(self.shc[:, l, c:c + 1], pkx[:, Tn:Tn + 1])
                else:
                    pk3 = pkx[:, 0:NS * 9].re("p (s t) -> p s t", t=9)
                    K.copy(pk3[:, :, 1:9], pt[:, :Tn].re("p (s t) -> p s t", t=8), eng=K.ACT)
                    K.copy(pk3[:, :, 0], shs[:, c, :])
                    d3 = dd[:, 0:Tn].re("p (s t) -> p s t", t=8)
                    K.tt(d3, pk3[:, :, 0:8], pk3[:, :, 1:9], ALU.subtract)
                    K.stt(xs_dst(c).re("p (s t) -> p s t", t=8), d3, mu[:, c:c + 1], pk3[:, :, 1:9], ALU.mult, ALU.add)
                    K.copy(sho[:, c, :], pk3[:, :, 8])
                c += 1
            self.done_w(1)
        if S:
            stg = self.S8[3]
            for c in range(14):
                pt = self.ps[6 + c % 2]
                pad = self.S8x[2][:, 0:128]
                K.copy(pad[:, 0:NS], sho[:, c, :])
                K.transpose(pt[:, 0:128], pad, self.ident)
                K.copy(stg[0:NS, c * 128:(c + 1) * 128], pt[0:NS, 0:128])
            K.dma(K.SP, self.dram["shift_s"][l], stg[0:NS, 0:1792], is_output=True)
        elif tiles[-1] == self.cfg.n_prompt_tiles - 1:
            stg = self.S8[3]
            pt = self.ps[6]
            pad = self.S8x[2][:, 0:128]
            K.copy(pad[:, 0:14], self.shc[:, l, :])
            K.transpose(pt[:, 0:128], pad, self.ident)
            K.copy(stg[0:14, 0:128], pt[0:14, 0:128])
            K.dma(K.SP, self.dram["shift_p"][l].rearrange("(c p) -> c p", p=128), stg[0:14, 0:128], is_output=True)
        ds = getattr(self.cfg, "dstage", 9)
        if ds < 2:
            K.memset(self.mT[:, 12:16, :Tn], 0.0)
            return
        for ti in range(nt):
            self.D_tile(blk, l, ti, S, XA, XB, (MU, MUI, ML, RESET, BONES))

    def D_tile(self, blk, l, ti, S, XA, XB, masks):
        K = self.K
        kind, tiles = blk
        MU, MUI, ML, RESET, BONES = masks
        cols = slice(ti * 128, (ti + 1) * 128)
        r_ = XA[:, 0:4, cols]
        k_ = XA[:, 4:8, cols]
        v_ = XB[:, 0:4, cols]
        f2 = self.S8[2].re("p (n c t) -> p n c t", n=4, c=4)
        f3 = self.S8[3].re("p (n c t) -> p n c t", n=4, c=4)
        LW, AA, KK, KP = f2[:, 0], f2[:, 1], f2[:, 2], f2[:, 3]
        GC, EG, GATE, BON = f3[:, 0], f3[:, 1], f3[:, 2], f3[:, 3]
        b4 = self.S8[4].bc(BF16)
        KtT = b4[:, 0:512].re("p (c t) -> p c t", c=4)
        RtT = b4[:, 512:1024].re("p (c t) -> p c t", c=4)
        khT = b4[:, 1024:1536].re("p (c t) -> p c t", c=4)
        bhT = b4[:, 1536:2048].re("p (c t) -> p c t", c=4)
        khtm = b4[:, 2048:2560]
        nbhtm = b4[:, 2560:3072]
        Vtm = b4[:, 3072:3584]
        sqb = b4[:, 3584:4096].re("p (c t) -> p c t", c=4)
        w0 = self.vcol("k_w0", l)
        a0 = self.vcol("k_a0", l)
        k_k = self.vcol("k_k_k", l)
        k_a = self.vcol("k_k_a", l)
        r_k = self.vcol("k_r_k", l)
        K.act(self.lt[0:64, 0, :], XB[0:64, 4, cols], AF.Tanh)
        K.copy(self.lt[64:128, 2, :], XB[64:128, 4, cols], eng=K.ACT)
        K.act(self.lt[:, 1, :], XB[:, 5, cols], AF.Sigmoid)
        for cc in range(4):
            pz = self.ps[cc % 2]
            K.mm(pz[:, 0:128], self.LWA[:, l, cc * 128:(cc + 1) * 128], self.lt[:, 0, :], start=True, stop=True, inc=False)
            K.mm(pz[:, 128:256], self.LWA[:, l, cc * 128:(cc + 1) * 128], self.lt[:, 2, :], start=True, stop=True, inc=False)
            K.mm(pz[:, 256:384], self.LG2[:, l, cc * 128:(cc + 1) * 128], self.lt[:, 1, :], start=True, stop=True)
            K.act(LW[:, cc, :], pz[:, 0:128], AF.Sigmoid, bias=w0[:, cc:cc + 1])
            K.act(AA[:, cc, :], pz[:, 128:256], AF.Sigmoid, bias=a0[:, cc:cc + 1])
            K.copy(GATE[:, cc, :], pz[:, 256:384], eng=K.ACT)
        K.ts(LW, LW, -float(np.exp(-0.5)), None, ALU.mult)
        for cc in range(4):
            K.ts(KK[:, cc, :], k_[:, cc, :], k_k[:, cc:cc + 1], None, ALU.mult)
        K.act(sqb, KK, AF.Square)
        pss = self.ps[2]
        for cc in range(4):
            K.mm(pss[:, cc * 128:(cc + 1) * 128], BONES, sqb[:, cc, :], start=True, stop=True, inc=(cc == 3))
        K.act(EG.re("p c t -> p (c t)"), pss, AF.Sqrt)
        K.ts(EG, EG, 1e-12, None, ALU.max)
        K.recip(EG.re("p c t -> p (c t)"), EG.re("p c t -> p (c t)"))
        K.tt(KK, KK, EG, ALU.mult)
        for cc in range(4):
            K.ts(KP[:, cc, :], AA[:, cc, :], -1.0, k_a[:, cc:cc + 1], ALU.add, ALU.mult)
        K.stt(KP, KP, 1.0, k_, ALU.add, ALU.mult)
        K.tt(AA, KK, AA, ALU.mult)
        K.tt(EG, r_, KP, ALU.mult)
        for cc in range(4):
            K.ts(sqb[:, cc, :], EG[:, cc, :], r_k[:, cc:cc + 1], None, ALU.mult)
        pbo = self.ps[3]
        for cc in range(4):
            K.mm(pbo[:, cc * 128:(cc + 1) * 128], BONES, sqb[:, cc, :], start=True, stop=True, inc=(cc == 3))
        K.tt(BON, pbo.re("p (c t) -> p c t", c=4), v_, ALU.mult)
        for cc in range(4):
            K.op(K.DVE, lambda cc=cc: self.nc.vector.tensor_tensor_scan(GC[:, cc, :].ap, RESET.ap, LW[:, cc, :].ap, 0.0, ALU.mult, ALU.add),
                 [RESET, LW], [GC])
        K.tt(LW, GC, LW, ALU.subtract)
        K.act(LW, LW, AF.Exp)
        K.tt(KtT, KK, LW, ALU.mult)
        K.act(EG, GC, AF.Exp)
        K.tt(RtT, r_, EG, ALU.mult)
        K.act(LW, GC, AF.Exp, scale=-1.0)
        K.tt(khT, KP, LW, ALU.mult)
        K.tt(bhT, AA, LW, ALU.mult)
        for (src, dst, neg) in ((khT, khtm, False), (bhT, nbhtm, True), (v_, Vtm, False)):
            ptb = self.psb[0] if dst is not nbhtm else self.psb[1]
            for cc in range(4):
                K.transpose(ptb[:, cc * 128:(cc + 1) * 128], src[:, cc, :], self.identb, inc=(cc == 3))
            if neg:
                K.ts(dst, ptb, -1.0, None, ALU.mult)
            else:
                K.copy(dst, ptb, eng=K.ACT)
        YT = KK
        ds = getattr(self.cfg, "dstage", 9)
        if ds < 3:
            K.memset(YT, 0.0)
        else:
            for cc in range(4):
                self.D_pair(blk, l, ti, cc, S, (KtT, RtT, khT, bhT, khtm, nbhtm, Vtm), (MU, MUI, ML), EG, YT)
        K.copy(sqb, YT, eng=K.ACT)
        pm = self.ps[2]
        for cc in range(4):
            K.mm(pm[:, cc * 128:(cc + 1) * 128], BONES, sqb[:, cc, :], start=True, stop=True, inc=(cc == 3))
        K.act(sqb, YT, AF.Square)
        pq = self.ps[3]
        for cc in range(4):
            K.mm(pq[:, cc * 128:(cc + 1) * 128], BONES, sqb[:, cc, :], start=True, stop=True, inc=(cc == 3))
        mean = KP.re("p c t -> p (c t)")
        K.ts(mean, pm, 1.0 / 64, None, ALU.mult)
        var = AA.re("p c t -> p (c t)")
        K.tt(var, mean, mean, ALU.mult)
        K.stt(var, pq, 1.0 / 64, var, ALU.mult, ALU.subtract)
        K.act(var, var, AF.Sqrt, bias=self.gnepsc, scale=1.0)
        K.recip(var, var)
        K.tt(YT, YT, KP, ALU.subtract)
        K.tt(YT, YT, AA, ALU.mult)
        lnw = self.vcol("k_ln_w", l)
        lnb = self.vcol("k_ln_b", l)
        for cc in range(4):
            K.ts(YT[:, cc, :], YT[:, cc, :], lnw[:, cc:cc + 1], lnb[:, cc:cc + 1], ALU.mult, ALU.add)
        K.tt(YT, YT, BON, ALU.add)
        K.tt(self.mT[:, 12:16, cols], YT, GATE, ALU.mult)

    def D_pair(self, blk, l, ti, cc, S, ops, masks, EG, YT):
        K = self.K
        kind, tiles = blk
        KtT, RtT, khT, bhT, khtm, nbhtm, Vtm = ops
        MU, MUI, ML = masks
        DM = self.DM
        MUb, MUIb, MLb = self.bcast_mid(MU, 2), self.bcast_mid(MUI, 2), self.bcast_mid(ML, 2)
        Ib = self.bcast_mid(self.identb, 2)
        BONES = self.dmask[:, 8, :]
        colm = self.dmx[:, 0:256].re("p (h t) -> p h t", h=2)
        hm = self.dmx[:, 256:258]
        rm = self.dmx32[:, 0:2]
        hmb = V(hm.ap.unsqueeze(2).broadcast_to([128, 2, 128]), hm.trk)
        pa, pb = self.ps[4], self.ps[5]
        H = [slice(0, 64), slice(64, 128)]
        cb = slice(cc * 128, (cc + 1) * 128)

        def v3(p, o):
            return p[:, o:o + 256].re("p (h t) -> p h t", h=2)

        def flat(t):
            return t.re("p h t -> p (h t)")
        KX, RX, BX = DM["KX"], DM["RX"], DM["BX"]
        K.tt(KX, self.bcast_mid(KtT[:, cc, :], 2), hmb, ALU.mult)
        K.tt(RX, self.bcast_mid(RtT[:, cc, :], 2), hmb, ALU.mult)
        K.tt(BX, self.bcast_mid(bhT[:, cc, :], 2), hmb, ALU.mult)
        K.mm(pa[:, 0:256], bhT[:, cc, :], flat(KX), start=True, stop=True)
        K.mm(pa[:, 256:512], KtT[:, cc, :], flat(BX), start=True, stop=True)
        K.stt(DM["X0"], v3(pa, 0), -1.0, MUb, ALU.mult, ALU.mult)
        K.stt(DM["XT0"], v3(pa, 256), -1.0, MLb, ALU.mult, ALU.mult)
        K.mm(pb[:, 0:256], khT[:, cc, :], flat(KX), start=True, stop=True)
        K.mm(pb[:, 256:512], khT[:, cc, :], flat(RX), start=True, stop=True)
        K.tt(DM["A2T"], v3(pb, 0), MUb, ALU.mult)
        K.tt(DM["A3T"], v3(pb, 256), MUIb, ALU.mult)
        K.mm(pa[:, 0:256], bhT[:, cc, :], flat(RX), start=True, stop=True)
        K.stt(DM["nA4T"], v3(pa, 0), -1.0, MUIb, ALU.mult, ALU.mult)
        K.tt(DM["PM"], DM["X0"], Ib, ALU.add)
        cur, nxt = ("X0", "XT0"), ("X1", "XT1")
        for d in range(getattr(self.cfg, "ndoub", 2 if S else 5)):
            X, XT = DM[cur[0]], DM[cur[1]]
            Xn, XTn = DM[nxt[0]], DM[nxt[1]]
            for h2 in range(2):
                K.mm(pb[:, h2 * 128:(h2 + 1) * 128], XT[:, h2, :], X[:, h2, :], start=True, stop=True)
                K.mm(pb[:, 256 + h2 * 128:256 + (h2 + 1) * 128], X[:, h2, :], XT[:, h2, :], start=True, stop=True)
            K.copy(Xn, v3(pb, 0))
            K.copy(XTn, v3(pb, 256))
            for h2 in range(2):
                K.mm(pa[:, h2 * 128:(h2 + 1) * 128], XTn[:, h2, :], DM["PM"][:, h2, :], start=True, stop=True)
            K.tt(DM["PM"], DM["PM"], v3(pa, 0), ALU.add)
            cur, nxt = nxt, cur
        PM, A2T, A3T, nA4T = DM["PM"], DM["A2T"], DM["A3T"], DM["nA4T"]
        if getattr(self.cfg, "dstage", 9) < 4:
            K.memset(YT[:, cc, :], 0.0)
            return
        Vx2, Ux2 = DM["VX"], DM["UX"]
        K.tt(Vx2, self.bcast_mid(Vtm[:, cb], 2), colm, ALU.mult)
        pW = self.ps[0][:, 0:128]
        pU = self.ps[0][:, 128:256]
        pY = self.ps[1][:, 0:128]
        pT = self.ps[1][:, 128:256]
        if not S:
            T32 = self.TD[:, l, cc, :]
            Tb = self.TDb[:, l, cc, :]
            for s in range(2):
                rows = slice(64 * s, 64 * s + 64)
                Vs = DM["VS"]
                K.ts(Vs, Vtm[:, cb], rm[:, s:s + 1], None, ALU.mult)
                K.mm(pW, KtT[:, cc, :], Tb, start=True, stop=False, inc=True)
                for h2 in range(2):
                    vc = slice(cc * 128 + 64 * h2, cc * 128 + 64 * h2 + 64)
                    K.mm(pW[:, H[h2]], A2T[:, h2, :], Vtm[:, vc], start=False, stop=(h2 == 1), inc=True)
                K.copy(self.Wb[rows, :], pW[rows, :], eng=K.ACT)
                for h2 in range(2):
                    K.mm(pU[:, H[h2]], PM[:, h2, :], self.Wb[:, H[h2]], start=True, stop=True)
                K.ts(self.Ub, pU, rm[:, s:s + 1], None, ALU.mult)
                K.tt(Ux2, self.bcast_mid(self.Ub, 2), colm, ALU.mult)
                K.mm(pY, Tb, RtT[:, cc, :], start=True, stop=False, inc=True)
                for h2 in range(2):
                    K.mm(pY, Vx2[:, h2, :], A3T[:, h2, :], start=False, stop=False, inc=True)
                    K.mm(pY, Ux2[:, h2, :], nA4T[:, h2, :], start=False, stop=(h2 == 1), inc=True)
                K.mm(pT, khtm[:, cb], Vs, start=True, stop=False, inc=True)
                K.mm(pT, nbhtm[:, cb], self.Ub, start=False, stop=True)
                K.copy(YT[:, cc, rows], pY[:, rows])
                K.tt(T32, T32, pT, ALU.add)
                K.stt(T32, T32, EG[:, cc, 64 * s + 63:64 * s + 64], BONES, ALU.mult, ALU.mult)
                K.copy(Tb, T32, eng=K.ACT)
            if tiles[ti] == self.cfg.n_prompt_tiles - 1:
                pt = self.ps[6 + cc % 2]
                K.transpose(pt[:, 0:128], T32, self.ident)
                so = self.rt[:, (cc % 2) * 128:(cc % 2 + 1) * 128]
                K.copy(so, pt[:, 0:128])
                for h2 in range(2):
                    K.dma(K.SP, self.dram["rwkv_p"][l, 2 * cc + h2], so[H[h2], H[h2]], is_output=True)
            return
        T0 = self.S8[1][:, 512:1536].re("p (s v) -> p s v", s=NS)
        T0blk = self.h[:, :, 128:256]
        Vx = self.h[:, :, 256:384]
        Sio = self.S8[0][:, 512:1024].re("p (s h c) -> p s h c", s=4, h=2)
        for half in range(2):
            for s2 in range(2):
                for h2 in range(2):
                    src = self.dram["state_rwkv"][l, 8 * half:8 * half + 8, 2 * cc + h2].rearrange("(s4 s2) v c -> s2 v s4 c", s2=2)[s2]
                    K.dma(K.SP, Sio[H[s2], :, h2, :], src)
            pt = self.ps[6 + half]
            for s4 in range(4):
                K.transpose(pt[:, s4 * 128:(s4 + 1) * 128], Sio[:, s4].re("p h c -> p (h c)"), self.ident)
            K.copy(T0[:, 8 * half:8 * half + 8, :].re("p s v -> p (s v)"), pt)
        K.memset(T0blk, 0.0)
        for h2 in range(2):
            K.copy(T0blk[H[h2], :, H[h2]], T0[H[h2], :, :], eng=(K.ACT if h2 else K.DVE))
        XTb = DM["VS"]
        for s in range(NS):
            K.mm(pY[:, 8 * s:8 * s + 8], T0blk[:, s, :], KtT[:, cc, 8 * s:8 * s + 8], start=True, stop=True)
        K.copy(XTb, pY, eng=K.ACT)
        K.mm(pW, XTb, self.identb, start=True, stop=False, inc=True)
        for h2 in range(2):
            vc = slice(cc * 128 + 64 * h2, cc * 128 + 64 * h2 + 64)
            K.mm(pW[:, H[h2]], A2T[:, h2, :], Vtm[:, vc], start=False, stop=(h2 == 1), inc=True)
        K.copy(self.Wb, pW, eng=K.ACT)
        for h2 in range(2):
            K.mm(pU[:, H[h2]], PM[:, h2, :], self.Wb[:, H[h2]], start=True, stop=True)
        K.copy(self.Ub, pU)
        K.tt(Ux2, self.bcast_mid(self.Ub, 2), colm, ALU.mult)
        pY2 = self.ps[1][:, 256:384]
        first = True
        for h2 in range(2):
            K.mm(pY2, Vx2[:, h2, :], A3T[:, h2, :], start=first, stop=False, inc=True)
            first = False
            K.mm(pY2, Ux2[:, h2, :], nA4T[:, h2, :], start=False, stop=False, inc=True)
        for s in range(NS):
            K.mm(pY2[:, 8 * s:8 * s + 8], T0blk[:, s, :], RtT[:, cc, 8 * s:8 * s + 8], start=False, stop=(s == NS - 1), inc=True)
        K.copy(YT[:, cc, :], pY2, eng=K.ACT)
        seqm = self.cst("SEQM")
        smb = V(seqm.ap.unsqueeze(2).broadcast_to([128, NS, 128]), seqm.trk)
        pTq = [self.ps[2], self.ps[3], self.ps[4], self.ps[5]]
        vin = V(Vtm.ap[:, cb].unsqueeze(1).broadcast_to([128, NS, 128]), Vtm.trk)
        K.tt(Vx, vin, smb, ALU.mult)
        for q in range(4):
            K.mm(pTq[q], khtm[:, cb], Vx[:, 4 * q:4 * q + 4, :], start=True, stop=False, inc=True)
        uin = V(self.Ub.ap.unsqueeze(1).broadcast_to([128, NS, 128]), self.Ub.trk)
        K.tt(Vx, uin, smb, ALU.mult)
        for q in range(4):
            K.mm(pTq[q], nbhtm[:, cb], Vx[:, 4 * q:4 * q + 4, :], start=False, stop=True)
        g8 = EG[:, cc, :].re("p (s t) -> p s t", t=8)[:, :, 7]
        for q in range(4):
            pv = pTq[q].re("p (s w) -> p s w", s=4)
            for h2 in range(2):
                g8b = V(g8.ap[H[h2], 4 * q:4 * q + 4].unsqueeze(2).broadcast_to([64, 4, 64]), g8.trk)
                K.tt(T0[H[h2], 4 * q:4 * q + 4, :], T0[H[h2], 4 * q:4 * q + 4, :], pv[H[h2], :, H[h2]], ALU.add)
                K.tt(T0[H[h2], 4 * q:4 * q + 4, :], T0[H[h2], 4 * q:4 * q + 4, :], g8b, ALU.mult)
        for half in range(2):
            pt = self.ps[6 + half]
            for s4 in range(4):
                s = 8 * half + 2 * s4
                K.transpose(pt[:, s4 * 128:(s4 + 1) * 128], T0[:, s:s + 2, :].re("p s v -> p (s v)"), self.ident)
            K.copy(Sio.re("p s h c -> p (s h c)"), pt)
            for s2 in range(2):
                for h2 in range(2):
                    dst = self.dram["rwkv_s"][l, 8 * half:8 * half + 8, 2 * cc + h2].rearrange("(s4 s2) v c -> s2 v s4 c", s2=2)[s2]
                    K.dma(K.SP, dst, Sio[H[s2], :, h2, :], is_output=True)


def make_in_map(inp, core, cfg, hc=None):
    hc = hc or host_consts()
    m = dict(hc)
    NP = cfg.n_prompt_tiles * 128
    b = core % 4
    m["xp"] = np.ascontiguousarray(inp["x_prompt"][b, :NP])
    for n in WEIGHT_SHAPES:
        m[n] = inp[n]
    for n, r in VEC_SPECS:
        m[n] = np.ascontiguousarray(inp[n].reshape(DEPTH, -1))
    m["final_norm"] = inp["final_norm"]
    m["a_w_s"] = inp["a_w_s"]
    m["a_b_s"] = inp["a_b_s"]
    for n in ("k_w2", "k_a2", "k_g2"):
        m[n] = inp[n]
    if cfg.do_sample:
        sl = slice(NS * core, NS * core + NS)
        m["xs"] = np.ascontiguousarray(inp["x_sample"][sl].reshape(NS * LS, D))
        m["state_ret"] = np.ascontiguousarray(inp["state_ret"][:, sl])
        m["state_conv"] = np.ascontiguousarray(inp["state_conv"][:, sl])
        m["state_rwkv_shift"] = np.ascontiguousarray(inp["state_rwkv_shift"][:, sl])
        m["state_rwkv"] = np.ascontiguousarray(inp["state_rwkv"][:, sl])
    return m


_OUT_GROUPS = "ABCD"


def kernel(**inputs):
    inp = {k: np.asarray(v) for k, v in inputs.items()}
    cfg = Cfg(groups=_OUT_GROUPS)
    B = Builder2(cfg)
    nc = B.build()
    hc = host_consts()
    in_maps = [make_in_map(inp, c, cfg, hc) for c in range(8)]
    res = run_bass_kernel_spmd(nc, in_maps, core_ids=list(range(8))).results
    BATCH, DEC_BATCH = 4, 128
    y_p = np.stack([res[b]["yp"] for b in range(BATCH)]).reshape(BATCH, SEQ, D)
    y_s = np.concatenate([res[c]["ys"].reshape(NS, LS, D) for c in range(8)], axis=0)

    def pstack(name, shp):
        if name not in res[0]:
            return np.zeros((DEPTH, BATCH) + shp, np.float32)
        return np.stack([res[b][name] for b in range(BATCH)], axis=1).reshape((DEPTH, BATCH) + shp)

    def sstack(name, shp):
        if name not in res[0]:
            return np.zeros((DEPTH, DEC_BATCH) + shp, np.float32)
        return np.concatenate([res[c][name].reshape((DEPTH, NS) + shp) for c in range(8)], axis=1)

    outs = (y_p.astype(np.float32), y_s.astype(np.float32),
            pstack("ret_p", (4, 128, 128)), sstack("ret_s", (4, 128, 128)),
            pstack("conv_p", (2, 512)), sstack("conv_s", (2, 512)),
            pstack("shift_p", (1792,)), sstack("shift_s", (1792,)),
            pstack("rwkv_p", (8, 64, 64)), sstack("rwkv_s", (8, 64, 64)),
            sstack("v_s", (LS, 512)))
    return outs
```
